# Optimizing a Trainium2 kernel written in Bass

```python
import math
import jax, jax.numpy as jnp
from jax import lax
import numpy as np

D_MODEL = 4096
BATCH = 1
SEQ = 8192
DEPTH = 2

HEAD_DIM = 128
ROPE_THETA = 10000.0
RMS_EPS = 1e-6
SUBLN_EPS = 1e-5

A_HEADS = D_MODEL // (2 * HEAD_DIM)
A_KV_HEADS = A_HEADS // 4
A_WINDOW = 128
A_BLOCK = 128

B_HEADS = D_MODEL // (2 * HEAD_DIM)
GRID_W = 64
NA_ROWS_MAX = 8
NA_COLS = 16

C_HEADS = D_MODEL // (2 * HEAD_DIM)
C_DIM = HEAD_DIM
C_BLOCK = 128

N_EVEN = (DEPTH + 1) // 2
N_ODD = DEPTH // 2

A_Q = A_HEADS * HEAD_DIM
A_KV = A_KV_HEADS * HEAD_DIM
B_W = B_HEADS * HEAD_DIM
EVEN_SPLITS = [A_Q, A_KV, A_KV, A_Q, B_W, B_W, B_W, B_W]
EVEN_IN = sum(EVEN_SPLITS)
EVEN_MIX = A_Q + B_W
C_W = C_HEADS * 2 * C_DIM
ODD_IN = 4 * C_W

kernel_name = "hybrid_window_natten_diffattn_encoder"


def _offsets(sizes):
    out, acc = [], 0
    for s in sizes[:-1]:
        acc += s
        out.append(acc)
    return out


def rms_norm(x, g, eps=RMS_EPS):
    xf = x.astype(jnp.float32)
    y = xf * lax.rsqrt(jnp.mean(xf * xf, axis=-1, keepdims=True) + eps)
    return (y * g.astype(jnp.float32)).astype(x.dtype)


def rope_tables(seq, dim):
    pos = jnp.arange(seq, dtype=jnp.float32)
    inv = 1.0 / (ROPE_THETA ** (jnp.arange(0, dim, 2, dtype=jnp.float32) / dim))
    ang = pos[:, None] * inv[None, :]
    ang = jnp.concatenate([ang, ang], axis=-1)
    return jnp.cos(ang), jnp.sin(ang)


def apply_rope(x, cos, sin):
    shape = (cos.shape[0],) + (1,) * (x.ndim - 3) + (cos.shape[1],)
    c, s = cos.reshape(shape), sin.reshape(shape)
    xf = x.astype(jnp.float32)
    half = x.shape[-1] // 2
    rot = jnp.concatenate([-xf[..., half:], xf[..., :half]], axis=-1)
    return (xf * c + rot * s).astype(x.dtype)


def window_attention(q, k, v, sink):
    b, s, h, d = q.shape
    hkv = k.shape[2]
    g = h // hkv
    nb = s // A_BLOCK
    qb = q.reshape(b, nb, A_BLOCK, hkv, g, d)
    pad = ((0, 0), (A_BLOCK, A_BLOCK), (0, 0), (0, 0))
    kp = jnp.pad(k, pad).reshape(b, nb + 2, A_BLOCK, hkv, d)
    vp = jnp.pad(v, pad).reshape(b, nb + 2, A_BLOCK, hkv, d)
    kb = jnp.concatenate([kp[:, :-2], kp[:, 1:-1], kp[:, 2:]], axis=2)
    vb = jnp.concatenate([vp[:, :-2], vp[:, 1:-1], vp[:, 2:]], axis=2)
    qpos = jnp.arange(s).reshape(nb, A_BLOCK)
    kpos = jnp.arange(-A_BLOCK, s + A_BLOCK).reshape(nb + 2, A_BLOCK)
    kposb = jnp.concatenate([kpos[:-2], kpos[1:-1], kpos[2:]], axis=1)
    valid = ((jnp.abs(qpos[:, :, None] - kposb[:, None, :]) <= A_WINDOW)
             & (kposb[:, None, :] >= 0) & (kposb[:, None, :] < s))
    scores = jnp.einsum('bnqkgd,bnckd->bnkgqc', qb, kb).astype(jnp.float32) * (d ** -0.5)
    scores = jnp.where(valid[None, :, None, None], scores, -1e30)
    sink_l = sink.astype(jnp.float32).reshape(hkv, g)[None, None, :, :, None, None]
    m = jnp.maximum(jnp.max(scores, axis=-1, keepdims=True), sink_l)
    p = jnp.exp(scores - m)
    denom = jnp.sum(p, axis=-1, keepdims=True) + jnp.exp(sink_l - m)
    probs = (p / denom).astype(v.dtype)
    out = jnp.einsum('bnkgqc,bnckd->bnqkgd', probs, vb)
    return out.reshape(b, s, h * d)


def neighborhood_attention(q, k, v, rpb):
    b, s, h, d = q.shape
    rows = s // GRID_W
    kr = min(NA_ROWS_MAX, rows)
    kc = NA_COLS
    qg = q.reshape(b, rows, GRID_W, h, d)
    kg = k.reshape(b, rows, GRID_W, h, d)
    vg = v.reshape(b, rows, GRID_W, h, d)
    col = jnp.arange(GRID_W)
    col_start = jnp.clip(col - kc // 2, 0, GRID_W - kc)
    col_idx = col_start[:, None] + jnp.arange(kc)[None, :]
    dc = col_idx - col[:, None]
    row_starts = jnp.clip(jnp.arange(rows) - kr // 2, 0, rows - kr)
    rpb_f = rpb.astype(jnp.float32)
    scale = d ** -0.5

    def one_row(args):
        r, rs = args
        q_r = lax.dynamic_index_in_dim(qg, r, axis=1, keepdims=False)
        k_rows = lax.dynamic_slice_in_dim(kg, rs, kr, axis=1)
        v_rows = lax.dynamic_slice_in_dim(vg, rs, kr, axis=1)
        k_nb = k_rows[:, :, col_idx]
        v_nb = v_rows[:, :, col_idx]
        sc = jnp.einsum('bwhd,biwjhd->bhwij', q_r, k_nb).astype(jnp.float32) * scale
        dr = rs + jnp.arange(kr) - r
        bias = rpb_f[:, dr[None, :, None] + NA_ROWS_MAX - 1,
                     dc[:, None, :] + NA_COLS - 1]
        sc = sc + bias[None]
        p = jax.nn.softmax(sc.reshape(b, h, GRID_W, kr * kc), axis=-1)
        p = p.reshape(b, h, GRID_W, kr, kc).astype(v.dtype)
        return jnp.einsum('bhwij,biwjhd->bwhd', p, v_nb)

    outs = lax.map(one_row, (jnp.arange(rows), row_starts))
    return outs.transpose(1, 0, 2, 3, 4).reshape(b, s, h * d)


def diff_attention(q, k, v, lam, subln_g, lambda_init):
    b, s, h, _, d = q.shape
    nb = s // C_BLOCK
    qb = q.reshape(b, nb, C_BLOCK, h, 2, d).transpose(1, 0, 2, 3, 4, 5)
    scale = d ** -0.5

    def one_block(qblk):
        sc = jnp.einsum('bqhtd,bkhtd->bhtqk', qblk, k).astype(jnp.float32) * scale
        p = jax.nn.softmax(sc, axis=-1)
        a = p[:, :, 0] - lam * p[:, :, 1]
        return jnp.einsum('bhqk,bkhe->bqhe', a.astype(v.dtype), v)

    o = lax.map(one_block, qb)
    o = o.transpose(1, 0, 2, 3, 4).reshape(b, s, h, 2 * d)
    o = (rms_norm(o, subln_g, SUBLN_EPS).astype(jnp.float32) * (1.0 - lambda_init)).astype(v.dtype)
    return o.reshape(b, s, h * 2 * d)


def even_layer(hn, w_in, sink, rpb, w_out, cos, sin):
    b, s, _ = hn.shape
    proj = jnp.einsum('bsd,de->bse', hn, w_in)
    qa, ka, va, za, qb, kb, vb, zb = jnp.split(proj, _offsets(EVEN_SPLITS), axis=-1)
    qa = apply_rope(qa.reshape(b, s, A_HEADS, HEAD_DIM), cos, sin)
    ka = apply_rope(ka.reshape(b, s, A_KV_HEADS, HEAD_DIM), cos, sin)
    va = va.reshape(b, s, A_KV_HEADS, HEAD_DIM)
    out_a = window_attention(qa, ka, va, sink) * jax.nn.silu(za)
    qb = qb.reshape(b, s, B_HEADS, HEAD_DIM)
    kb = kb.reshape(b, s, B_HEADS, HEAD_DIM)
    vb = vb.reshape(b, s, B_HEADS, HEAD_DIM)
    out_b = neighborhood_attention(qb, kb, vb, rpb) * jax.nn.silu(zb)
    mixed = jnp.concatenate([out_a, out_b], axis=-1)
    return jnp.einsum('bse,ed->bsd', mixed, w_out)


def odd_layer(hn, w_in, lq1, lk1, lq2, lk2, subln_g, w_out, cos, sin, lambda_init):
    b, s, _ = hn.shape
    proj = jnp.einsum('bsd,de->bse', hn, w_in)
    q, k, v, z = jnp.split(proj, _offsets([C_W, C_W, C_W, C_W]), axis=-1)
    q = apply_rope(q.reshape(b, s, C_HEADS, 2, C_DIM), cos, sin)
    k = apply_rope(k.reshape(b, s, C_HEADS, 2, C_DIM), cos, sin)
    v = v.reshape(b, s, C_HEADS, 2 * C_DIM)
    f32 = jnp.float32
    lam = (jnp.exp(jnp.sum(lq1.astype(f32) * lk1.astype(f32)))
           - jnp.exp(jnp.sum(lq2.astype(f32) * lk2.astype(f32))) + lambda_init)
    out = diff_attention(q, k, v, lam, subln_g, lambda_init) * jax.nn.silu(z)
    return jnp.einsum('bse,ed->bsd', out, w_out)


def setup_inputs(seed: int = 0) -> dict:
    key = jax.random.key(seed)
    ks = jax.random.split(key, 16)
    f32 = jnp.float32
    nrm = lambda k, shape, scale: jax.random.normal(k, shape, f32) * scale
    return {
        "x": nrm(ks[0], (BATCH, SEQ, D_MODEL), 1.0),
        "norm_g": 1.0 + nrm(ks[1], (DEPTH, D_MODEL), 0.02),
        "w_in_even": nrm(ks[2], (N_EVEN, D_MODEL, EVEN_IN), D_MODEL ** -0.5),
        "sink_a": nrm(ks[3], (N_EVEN, A_HEADS), 0.5),
        "rpb_b": nrm(ks[4], (N_EVEN, B_HEADS, 2 * NA_ROWS_MAX - 1, 2 * NA_COLS - 1), 0.1),
        "w_out_even": nrm(ks[5], (N_EVEN, EVEN_MIX, D_MODEL), EVEN_MIX ** -0.5),
        "w_in_odd": nrm(ks[6], (N_ODD, D_MODEL, ODD_IN), D_MODEL ** -0.5),
        "lambda_q1": nrm(ks[7], (N_ODD, C_DIM), 0.1),
        "lambda_k1": nrm(ks[8], (N_ODD, C_DIM), 0.1),
        "lambda_q2": nrm(ks[9], (N_ODD, C_DIM), 0.1),
        "lambda_k2": nrm(ks[10], (N_ODD, C_DIM), 0.1),
        "subln_g": 1.0 + nrm(ks[11], (N_ODD, 2 * C_DIM), 0.02),
        "w_out_odd": nrm(ks[12], (N_ODD, C_W, D_MODEL), C_W ** -0.5),
        "final_g": 1.0 + nrm(ks[13], (D_MODEL,), 0.02),
    }


def reference(x, norm_g, w_in_even, sink_a, rpb_b, w_out_even, w_in_odd,
              lambda_q1, lambda_k1, lambda_q2, lambda_k2, subln_g, w_out_odd, final_g):
    s = x.shape[1]
    cos, sin = rope_tables(s, HEAD_DIM)
    h = x
    for layer in range(DEPTH):
        hn = rms_norm(h, norm_g[layer])
        i = layer // 2
        if layer % 2 == 0:
            h = h + even_layer(hn, w_in_even[i], sink_a[i], rpb_b[i], w_out_even[i], cos, sin)
        else:
            lambda_init = 0.8 - 0.6 * math.exp(-0.3 * layer)
            h = h + odd_layer(hn, w_in_odd[i], lambda_q1[i], lambda_k1[i], lambda_q2[i],
                              lambda_k2[i], subln_g[i], w_out_odd[i], cos, sin, lambda_init)
    return rms_norm(h, final_g)
```

```python
import contextlib
import math
import numpy as np
import ml_dtypes
import concourse.bass as bass
import concourse.mybir as mybir
from concourse.bass_utils import run_bass_kernel_spmd

F32 = mybir.dt.float32
BF16 = mybir.dt.bfloat16
AF = mybir.ActivationFunctionType
ALU = mybir.AluOpType

NCORES = 8
D = 4096
SEQ = 8192
TOK = SEQ // NCORES
HALO = 256
EXT = TOK + 2 * HALO
NDC = D // 128
RMS_EPS = 1e-6
SUBLN_EPS = 1e-5
SCALE = 128 ** -0.5
NEG = -1e30
LAMBDA_INIT = 0.8 - 0.6 * math.exp(-0.3 * 1)

ENGINES = ("pe", "act", "dve", "pool", "sp")
_STOP = 99
_DBG = False


class Buf:
    __slots__ = ("name", "last_w", "readers", "dma_cnt", "sem", "base")

    def __init__(self, name):
        self.name = name
        self.last_w = None
        self.readers = []
        self.dma_cnt = 0
        self.sem = None
        self.base = 0


class Op:
    __slots__ = ("eng", "fn", "deps", "is_dma", "idx", "signal", "token_sem", "token_val", "owner", "sched")


_BANK = {}


class SemBank:
    def __init__(self, nc, stack):
        self.nc, self.stack = nc, stack
        self.eng = {}
        self.eng_cnt = {}
        self.dma = []
        self.dma_cnt = []

    def eng_sem(self, e):
        if e not in self.eng:
            self.eng[e] = self.stack.enter_context(self.nc.semaphore("s_" + e))
            self.eng_cnt[e] = 0
        return self.eng[e]

    def dma_sem(self, i):
        while len(self.dma) <= i:
            self.dma.append(self.stack.enter_context(self.nc.semaphore("d%d" % len(self.dma))))
            self.dma_cnt.append(0)
        return self.dma[i]


class Sched:
    def __init__(self, nc):
        self.nc = nc
        self.bank = _BANK[id(nc)]
        self.ops = {e: [] for e in ENGINES}
        self.dma_owners = []

    def _mk(self, eng, fn, reads, writes, is_dma, owner):
        op = Op()
        op.eng, op.fn, op.is_dma, op.owner = eng, fn, is_dma, owner
        op.signal = False
        op.token_sem = None
        op.token_val = 0
        deps = []
        for b in reads:
            if b.last_w is not None:
                deps.append(b.last_w)
        for b in writes:
            lw = b.last_w
            if lw is not None and not (is_dma and lw.is_dma and lw.owner is owner):
                deps.append(lw)
            deps.extend(b.readers)
        op.deps = [d for d in deps if d.sched is self]
        op.sched = self
        for b in reads:
            b.readers.append(op)
        for b in writes:
            b.last_w = op
            b.readers = []
        op.idx = len(self.ops[eng])
        self.ops[eng].append(op)
        if is_dma:
            owner.dma_cnt += 1
            op.token_val = 16 * owner.dma_cnt
            if owner not in self.dma_owners:
                self.dma_owners.append(owner)
        return op

    def op(self, eng, fn, reads=(), writes=()):
        return self._mk(eng, fn, list(reads), list(writes), False, None)

    def dma(self, eng, fn, reads=(), writes=(), owner=None):
        assert owner is not None
        return self._mk(eng, fn, list(reads), list(writes), True, owner)

    def emit(self, final_wait_ops=()):
        nc = self.nc
        waits = {e: [] for e in ENGINES}
        for e in ENGINES:
            seen_eng = {p: -1 for p in ENGINES}
            seen_dma = {}
            for op in self.ops[e]:
                need_eng = {}
                need_dma = {}
                for d in op.deps:
                    if d.is_dma:
                        if seen_dma.get(d.owner, 0) < d.token_val:
                            if need_dma.get(d.owner, 0) < d.token_val:
                                need_dma[d.owner] = d.token_val
                    else:
                        if d.eng == e and e == "pe":
                            continue
                        if seen_eng[d.eng] < d.idx:
                            prev = need_eng.get(d.eng)
                            if prev is None or prev.idx < d.idx:
                                need_eng[d.eng] = d
                w = []
                for pe_, d in need_eng.items():
                    seen_eng[pe_] = d.idx
                    d.signal = True
                    w.append((d,))
                for owner, val in need_dma.items():
                    seen_dma[owner] = val
                    w.append((owner, val))
                waits[e].append(w)
        last_ops = []
        for e in ENGINES:
            comp = [o for o in self.ops[e] if not o.is_dma]
            if comp:
                comp[-1].signal = True
                last_ops.append(comp[-1])
        all_dma_final = {}
        for e in ENGINES:
            for o in self.ops[e]:
                if o.is_dma:
                    all_dma_final[o.owner] = max(all_dma_final.get(o.owner, 0), o.token_val)
        with contextlib.ExitStack() as st:
            bank = self.bank
            esem = {e: bank.eng_sem(e) for e in ENGINES if e != "sp"}
            for i, owner in enumerate(self.dma_owners):
                owner.sem = bank.dma_sem(i)
                owner.base = bank.dma_cnt[i]
                bank.dma_cnt[i] += 16 * owner.dma_cnt
            for e in ENGINES:
                cnt = bank.eng_cnt.get(e, 0)
                for op in self.ops[e]:
                    if op.is_dma:
                        op.token_sem = op.owner.sem
                    elif op.signal:
                        cnt += 1
                        op.token_sem = esem[e]
                        op.token_val = cnt
                if e != "sp":
                    bank.eng_cnt[e] = cnt
            block = st.enter_context(nc.Block())
            engmap = {"pe": block.tensor, "act": block.scalar, "dve": block.vector,
                      "pool": block.gpsimd, "sp": block.sync}

            def make(e):
                def body(eng):
                    for op, w in zip(self.ops[e], waits[e]):
                        for item in w:
                            if len(item) == 1:
                                eng.wait_ge(item[0].token_sem, item[0].token_val)
                            else:
                                eng.wait_ge(item[0].sem, item[0].base + item[1])
                        ins = op.fn(eng)
                        if op.is_dma:
                            ins.then_inc(op.token_sem, 16)
                        elif op.signal:
                            ins.then_inc(op.token_sem, 1)
                    for lo in last_ops:
                        eng.wait_ge(lo.token_sem, lo.token_val)
                    for owner, val in all_dma_final.items():
                        eng.wait_ge(owner.sem, owner.base + val)
                return body

            for e in ENGINES:
                engmap[e](make(e))


def _even_col_order():
    cols = []
    for g in range(4):
        cols += list(range(2048 + g * 128, 2048 + (g + 1) * 128))
        cols += list(range(2560 + g * 128, 2560 + (g + 1) * 128))
        for h in range(4 * g, 4 * g + 4):
            cols += list(range(h * 128, (h + 1) * 128))
            cols += list(range(3072 + h * 128, 3072 + (h + 1) * 128))
    for h in range(16):
        cols += list(range(7168 + h * 128, 7168 + (h + 1) * 128))
        cols += list(range(9216 + h * 128, 9216 + (h + 1) * 128))
        cols += list(range(5120 + h * 128, 5120 + (h + 1) * 128))
        cols += list(range(11264 + h * 128, 11264 + (h + 1) * 128))
    return np.asarray(cols, dtype=np.int64)


def _slabs(w, sc):
    c = w.shape[1]
    ns = c // sc
    return np.ascontiguousarray(w.reshape(NDC, 128, ns, sc).transpose(2, 1, 0, 3)).reshape(ns, 128, NDC * sc)


def _rope_tables():
    pos = np.arange(SEQ, dtype=np.float32)
    inv = (1.0 / (np.float32(10000.0) ** (np.arange(0, 128, 2, dtype=np.float32) / np.float32(128)))).astype(np.float32)
    ang = (pos[:, None] * inv[None, :]).astype(np.float32)
    ang = np.concatenate([ang, ang], axis=-1)
    cos = np.cos(ang).astype(np.float32)
    sin = np.sin(ang).astype(np.float32)
    sgn = np.concatenate([-np.ones(64, np.float32), np.ones(64, np.float32)])
    return cos, sin * sgn[None, :]


def _window_mask(core):
    m = np.full((128, 8, 3, 128), NEG, np.float32)
    ki = np.arange(128)[:, None]
    qi = np.arange(128)[None, :]
    for qb in range(8):
        for j in range(3):
            diff = (j - 1) * 128 + ki - qi
            tk = core * TOK + qb * 128 + qi + diff
            ok = (np.abs(diff) <= 128) & (tk >= 0) & (tk < SEQ)
            m[:, qb, j, :] = np.where(ok, 0.0, NEG)
    return m.reshape(128, 8 * 384)


def _natten_tables(core, rpb):
    ki = np.arange(128)[:, None]
    qi = np.arange(128)[None, :]
    mask = np.full((128, 8, 7, 128), NEG, np.float32)
    bias = np.zeros((16, 128, 7, 128), np.float32)
    for j in range(7):
        dr = (j - 3) * 2 + ki // 64 - qi // 64
        dc = ki % 64 - qi % 64
        inb = (np.abs(dr) <= 7) & (np.abs(dc) <= 15)
        bias[:, :, j, :] = np.where(inb[None], rpb[:, np.clip(dr + 7, 0, 14), np.clip(dc + 15, 0, 30)], 0.0)
        for qb in range(8):
            tq = core * TOK + qb * 128 + qi
            tk = tq + (j - 3) * 128 + ki - qi
            rq = tq // 64
            cq = tq % 64
            rk = tk // 64
            ck = tk % 64
            rs = np.clip(rq - 4, 0, 120)
            cs = np.clip(cq - 8, 0, 48)
            ok = (tk >= 0) & (tk < SEQ) & (rk >= rs) & (rk < rs + 8) & (ck >= cs) & (ck < cs + 16)
            mask[:, qb, j, :] = np.where(ok, 0.0, NEG)
    return mask.reshape(128, 8 * 896), bias.reshape(16, 128, 896)


def _bcast(v, n=128):
    v = np.asarray(v, np.float32).reshape(1, -1)
    return np.ascontiguousarray(np.broadcast_to(v, (n, v.shape[1])))


class Ctx:
    pass


_UNIQ = [0]


def _alloc(nc, st):
    def T(name, shape, dt):
        _UNIQ[0] += 1
        return st.enter_context(nc.sbuf_tensor("sb%d_%s" % (_UNIQ[0], name), list(shape), dt))

    def P(name, shape, dt):
        _UNIQ[0] += 1
        return st.enter_context(nc.psum_tensor("ps%d_%s" % (_UNIQ[0], name), list(shape), dt))
    return T, P


def rmsnorm_transpose_tile(S, c, xt, bxt, hnb, bhnb, gb, bg, ssq, sd, rstd, bstat, col, identb, bident,
                           ptr, bptr, dst_fn, bdst, sqj, bsqj, eps, cnt):
    S.op("act", lambda g: g.activation(out=sqj[:], in_=xt[:], func=AF.Square, accum_out=ssq[:, col:col + 1]),
         reads=[bxt], writes=[bsqj, bstat])
    S.op("act", lambda g: g.activation(out=sd[:, col:col + 1], in_=ssq[:, col:col + 1], func=AF.Sqrt,
                                       scale=1.0 / D, bias=eps), reads=[bstat], writes=[bstat])
    S.op("dve", lambda g: g.reciprocal(out=rstd[:, col:col + 1], in_=sd[:, col:col + 1]), reads=[bstat], writes=[bstat])
    S.op("dve", lambda g: g.scalar_tensor_tensor(out=hnb[:], in0=xt[:], scalar=rstd[:, col:col + 1], in1=gb[:],
                                                 op0=ALU.mult, op1=ALU.mult),
         reads=[bxt, bstat, bg], writes=[bhnb])
    for grp in range(4):
        pb = (cnt[0]) % 2
        cnt[0] += 1
        for i in range(8):
            dc = grp * 8 + i
            S.op("pe", lambda g, dc=dc, i=i, pb=pb: g.transpose(out=ptr[pb][:, i * 128:(i + 1) * 128],
                                                                in_=hnb[:, dc * 128:(dc + 1) * 128], identity=identb[:]),
                 reads=[bhnb, bident], writes=[bptr[pb]])
        src = ptr[pb][:].rearrange("p (a b) -> p a b", a=8)
        if grp % 2 == 0:
            S.op("act", lambda g, grp=grp, src=src: g.copy(out=dst_fn(grp), in_=src), reads=[bptr[pb]], writes=[bdst])
        else:
            S.op("dve", lambda g, grp=grp, src=src: g.tensor_copy(out=dst_fn(grp), in_=src), reads=[bptr[pb]], writes=[bdst])


def proj_tile(S, wslab, bw, ft, sc, hnT_fn, bhn, pacc, bpacc, nblk):
    for dc in range(NDC):
        for blk in range(nblk):
            S.op("pe", lambda g, dc=dc, blk=blk: g.matmul(pacc[blk][:], lhsT=wslab[:, dc * sc + ft * 128: dc * sc + ft * 128 + 128],
                                                         rhs=hnT_fn(dc, blk), start=(dc == 0), stop=(dc == NDC - 1)),
                 reads=[bw, bhn], writes=[bpacc[blk]])


def rope_block(S, src_ps, bsrc, xf, bxf, permf, bperm, rot, brot, cosb, sinb, btab, t1, bt1, t2, bt2, dst, bdst):
    S.op("act", lambda g: g.copy(out=xf, in_=src_ps), reads=[bsrc], writes=[bxf])
    S.op("pe", lambda g: g.matmul(rot, lhsT=permf[:], rhs=xf, start=True, stop=True), reads=[bperm, bxf], writes=[brot])
    S.op("pool", lambda g: g.tensor_tensor(out=t1, in0=xf, in1=cosb, op=ALU.mult), reads=[bxf, btab], writes=[bt1])
    S.op("dve", lambda g: g.tensor_tensor(out=t2, in0=rot, in1=sinb, op=ALU.mult), reads=[brot, btab], writes=[bt2])
    S.op("dve", lambda g: g.tensor_tensor(out=dst, in0=t1, in1=t2, op=ALU.add), reads=[bt1, bt2], writes=[bdst])


def _mixer_scope(nc, io, is_A, hnT, bhnT, identb, bident):
    SC = 256
    W_ = 384 if is_A else 896
    with contextlib.ExitStack() as st:
        T, P = _alloc(nc, st)
        S = Sched(nc)
        wsl = [T("wsl%d" % i, [128, NDC * SC], BF16) for i in range(2)]
        bwsl = [Buf("wsl0"), Buf("wsl1")]
        ones = T("ones", [128, 128], BF16)
        kT = T("kT", [128, EXT], BF16)
        vT = T("vT", [128, EXT], BF16)
        V = T("V", [128, EXT // 128, 128], BF16)
        qT = T("qT", [128, TOK], BF16)
        zs = T("zs", [128, TOK], F32)
        Sm = [T("Sm%d" % i, [128, W_], F32) for i in range(2)]
        Pt = [T("Pt%d" % i, [128, W_], BF16) for i in range(2)]
        den = [T("den%d" % i, [128, 128], F32) for i in range(2)]
        rec = [T("rec%d" % i, [128, 128], F32) for i in range(2)]
        mo = [T("mo%d" % i, [128, 128], F32) for i in range(2)]
        mx = [T("mx%d" % i, [128, TOK], BF16) for i in range(2)]
        PA = [P("PA%d" % i, [128, 512], F32) for i in range(3)]
        ROT = P("ROT", [128, 512], F32)
        TRP = P("TRP", [128, 1024], BF16)
        Sps = P("Sps", [128, 1024], F32)
        OS = P("OS", [128, 512], F32)
        bPA = [Buf("PA%d" % i) for i in range(3)]
        bROT, bTRP, bSps = Buf("ROT"), Buf("TRP"), Buf("Sps")
        bOS = [Buf("OS0"), Buf("OS1")]
        bones = Buf("ones")
        bkT, bvT, bV, bqT, bzs = (Buf(n) for n in ["kT", "vT", "V", "qT", "zs"])
        bSm = [Buf("Sm0"), Buf("Sm1")]
        bPt = [Buf("Pt0"), Buf("Pt1")]
        bden = [Buf("den0"), Buf("den1")]
        brec = [Buf("rec0"), Buf("rec1")]
        bmo = [Buf("mo0"), Buf("mo1")]
        bmx = [Buf("mx0"), Buf("mx1")]
        bmix = Buf("mix_s")
        S.op("pool", lambda g: g.memset(ones[:], 1.0), writes=[bones])
        if is_A:
            permf = T("permf", [128, 128], F32)
            cosT = T("cosT", [128, EXT], F32)
            sinT = T("sinT", [128, EXT], F32)
            maskW = T("maskWs", [128, 8 * 384], BF16)
            sinkb = T("sinkbs", [128, 16], F32)
            esink = T("esink", [128, 16], F32)
            kf = T("kf", [128, 512], F32)
            t1 = T("t1", [128, 512], F32)
            t2 = T("t2", [128, 512], F32)
            bperm, bcos, bsin, bmW, bsink, besink = (Buf(n) for n in ["perm", "cos", "sin", "maskW", "sink", "esink"])
            bkf, bt1, bt2 = Buf("kf"), Buf("t1"), Buf("t2")
            S.dma("sp", lambda g: g.dma_start(out=permf[:], in_=io["perm"][:, :]), writes=[bperm], owner=bperm)
            S.dma("sp", lambda g: g.dma_start(out=cosT[:], in_=io["cosA"][:, :]), writes=[bcos], owner=bcos)
            S.dma("sp", lambda g: g.dma_start(out=sinT[:], in_=io["sinA"][:, :]), writes=[bsin], owner=bsin)
            S.dma("sp", lambda g: g.dma_start(out=maskW[:], in_=io["maskW"][:, :]), writes=[bmW], owner=bmW)
            S.dma("sp", lambda g: g.dma_start(out=sinkb[:], in_=io["sinkb"][:, :]), writes=[bsink], owner=bsink)
            S.op("act", lambda g: g.activation(out=esink[:], in_=sinkb[:], func=AF.Exp), reads=[bsink], writes=[besink])
        else:
            maskB = T("maskBs", [128, 8 * 896], BF16)
            biasB = [T("biasBs%d" % i, [128, 896], F32) for i in range(2)]
            bm = [T("bm%d" % i, [128, 896], F32) for i in range(2)]
            bmB = Buf("maskB")
            bbias = [Buf("biasB0"), Buf("biasB1")]
            bbm = [Buf("bm0"), Buf("bm1")]
            S.dma("sp", lambda g: g.dma_start(out=maskB[:], in_=io["maskB"][:, :]), writes=[bmB], owner=bmB)

        slab_state = {"n": 0 if is_A else 20}
        slab_end = 20 if is_A else 52

        def load_slab():
            s_ = slab_state["n"]
            b = s_ % 2
            S.dma("pool", lambda g, s_=s_, b=b: g.dma_start(out=wsl[b][:], in_=io["w0r"][s_]), writes=[bwsl[b]], owner=bwsl[b])
            slab_state["n"] += 1
            return wsl[b], bwsl[b]

        pending = [load_slab()]

        def next_slab():
            cur = pending.pop(0)
            if slab_state["n"] < slab_end:
                pending.append(load_slab())
            return cur

        hn_ext = lambda dc, blk: hnT[:, dc, blk * 512:(blk + 1) * 512]
        hn_own = lambda dc, blk: hnT[:, dc, HALO + blk * 512: HALO + (blk + 1) * 512]

        def rope(src, bsrc, sl_tab, dst, bdst):
            S.op("act", lambda g: g.copy(out=kf[:], in_=src), reads=[bsrc], writes=[bkf])
            S.op("pe", lambda g: g.matmul(ROT[:], lhsT=permf[:], rhs=kf[:], start=True, stop=True), reads=[bperm, bkf], writes=[bROT])
            S.op("pool", lambda g: g.tensor_tensor(out=t1[:], in0=kf[:], in1=cosT[:, sl_tab], op=ALU.mult), reads=[bkf, bcos], writes=[bt1])
            S.op("dve", lambda g: g.tensor_tensor(out=t2[:], in0=ROT[:], in1=sinT[:, sl_tab], op=ALU.mult), reads=[bROT, bsin], writes=[bt2])
            S.op("dve", lambda g: g.tensor_tensor(out=dst, in0=t1[:], in1=t2[:], op=ALU.add), reads=[bt1, bt2], writes=[bdst])

        def evac_kv(w, bw_):
            proj_tile(S, w, bw_, 0, SC, hn_ext, bhnT, PA, bPA, 3)
            for blk in range(3):
                sl = slice(blk * 512, (blk + 1) * 512)
                if is_A:
                    rope(PA[blk][:], bPA[blk], sl, kT[:, sl], bkT)
                else:
                    S.op("act", lambda g, blk=blk, sl=sl: g.copy(out=kT[:, sl], in_=PA[blk][:]), reads=[bPA[blk]], writes=[bkT])
            proj_tile(S, w, bw_, 1, SC, hn_ext, bhnT, PA, bPA, 3)
            for blk in range(3):
                sl = slice(blk * 512, (blk + 1) * 512)
                S.op("act", lambda g, blk=blk, sl=sl: g.copy(out=vT[:, sl], in_=PA[blk][:]), reads=[bPA[blk]], writes=[bvT])
            for (t0, nt) in [(0, 8), (8, 4)]:
                for i in range(nt):
                    tt = t0 + i
                    S.op("pe", lambda g, tt=tt, i=i: g.transpose(out=TRP[:, i * 128:(i + 1) * 128], in_=vT[:, tt * 128:(tt + 1) * 128],
                                                                 identity=identb[:]), reads=[bvT, bident], writes=[bTRP])
                S.op("dve", lambda g, t0=t0, nt=nt: g.tensor_copy(out=V[:, t0:t0 + nt, :],
                                                                 in_=TRP[:, 0:nt * 128].rearrange("p (a b) -> p a b", a=nt)),
                     reads=[bTRP], writes=[bV])

        def evac_qz(w, bw_):
            proj_tile(S, w, bw_, 0, SC, hn_own, bhnT, PA, bPA, 2)
            for blk in range(2):
                sl = slice(blk * 512, (blk + 1) * 512)
                sle = slice(HALO + blk * 512, HALO + (blk + 1) * 512)
                if is_A:
                    rope(PA[blk][:], bPA[blk], sle, qT[:, sl], bqT)
                else:
                    S.op("act", lambda g, blk=blk, sl=sl: g.copy(out=qT[:, sl], in_=PA[blk][:]), reads=[bPA[blk]], writes=[bqT])
            proj_tile(S, w, bw_, 1, SC, hn_own, bhnT, PA, bPA, 2)
            for blk in range(2):
                sl = slice(blk * 512, (blk + 1) * 512)
                S.op("act", lambda g, blk=blk, sl=sl: g.activation(out=zs[:, sl], in_=PA[blk][:], func=AF.Silu),
                     reads=[bPA[blk]], writes=[bzs])

        itc = [0]

        def attend(qb, jlist, e_of_j, add_fn, denom_fn, mxb, bmxb):
            b = itc[0] % 2
            itc[0] += 1
            j0 = jlist[0]
            n = len(jlist)
            cs = slice(j0 * 128, (j0 + n) * 128)
            for j in jlist:
                e = e_of_j(j)
                S.op("pe", lambda g, j=j, e=e: g.matmul(Sps[:, j * 128:(j + 1) * 128], lhsT=kT[:, e * 128:(e + 1) * 128],
                                                       rhs=qT[:, qb * 128:(qb + 1) * 128], start=True, stop=True),
                     reads=[bkT, bqT], writes=[bSps])
            add_fn(b, cs)
            S.op("act", lambda g: g.activation(out=Pt[b][:, cs], in_=Sm[b][:, cs], func=AF.Exp),
                 reads=[bSm[b]], writes=[bPt[b]])
            oc = slice(b * 256, b * 256 + 128)
            sc_ = slice(b * 256 + 128, b * 256 + 256)
            for idx, j in enumerate(jlist):
                e = e_of_j(j)
                S.op("pe", lambda g, j=j, e=e, idx=idx: g.matmul(OS[:, oc], lhsT=V[:, e, :], rhs=Pt[b][:, j * 128:(j + 1) * 128],
                                                                start=(idx == 0), stop=(idx == n - 1)),
                     reads=[bV, bPt[b]], writes=[bOS[b]])
            for idx, j in enumerate(jlist):
                S.op("pe", lambda g, j=j, idx=idx: g.matmul(OS[:, sc_], lhsT=ones[:], rhs=Pt[b][:, j * 128:(j + 1) * 128],
                                                           start=(idx == 0), stop=(idx == n - 1)),
                     reads=[bones, bPt[b]], writes=[bOS[b]])
            denom_fn(b, sc_)
            S.op("dve", lambda g: g.reciprocal(out=rec[b][:], in_=den[b][:]), reads=[bden[b]], writes=[brec[b]])
            S.op("dve", lambda g: g.tensor_tensor(out=mo[b][:], in0=OS[:, oc], in1=rec[b][:], op=ALU.mult),
                 reads=[bOS[b], brec[b]], writes=[bmo[b]])
            S.op("pool", lambda g: g.tensor_tensor(out=mxb[:, qb * 128:(qb + 1) * 128], in0=mo[b][:],
                                                   in1=zs[:, qb * 128:(qb + 1) * 128], op=ALU.mult),
                 reads=[bmo[b], bzs], writes=[bmxb])

        mxc = [0]
        if is_A:
            for g4 in range(4):
                w, bw_ = next_slab()
                evac_kv(w, bw_)
                for hh in range(4):
                    h = 4 * g4 + hh
                    w, bw_ = next_slab()
                    evac_qz(w, bw_)
                    mi = mxc[0] % 2
                    mxc[0] += 1
                    for qb in range(8):
                        def add_fn(b, cs, qb=qb):
                            S.op("dve", lambda g: g.scalar_tensor_tensor(out=Sm[b][:, cs], in0=Sps[:, cs], scalar=SCALE,
                                                                         in1=maskW[:, qb * 384:(qb + 1) * 384],
                                                                         op0=ALU.mult, op1=ALU.add),
                                 reads=[bSps, bmW], writes=[bSm[b]])

                        def denom_fn(b, sc_, h=h):
                            S.op("dve", lambda g: g.tensor_scalar(out=den[b][:], in0=OS[:, sc_], scalar1=esink[:, h:h + 1],
                                                                  scalar2=None, op0=ALU.add),
                                 reads=[bOS[b], besink], writes=[bden[b]])
                        attend(qb, [0, 1, 2], lambda j, qb=qb: qb + 1 + j, add_fn, denom_fn, mx[mi], bmx[mi])
                    S.dma("sp", lambda g, h=h, mi=mi: g.dma_start(out=io["mix_s"][h], in_=mx[mi][:]),
                          reads=[bmx[mi]], writes=[bmix], owner=bmx[mi])
                    if "mix_dbg" in io:
                        S.dma("sp", lambda g, h=h, mi=mi: g.dma_start(out=io["mix_dbg"][h], in_=mx[mi][:]),
                              reads=[bmx[mi]], writes=[bmix], owner=bmx[mi])
        else:
            for h in range(16):
                bi = h % 2
                S.dma("sp", lambda g, h=h, bi=bi: g.dma_start(out=biasB[bi][:], in_=io["biasB"][h]), writes=[bbias[bi]], owner=bbias[bi])
                w, bw_ = next_slab()
                evac_kv(w, bw_)
                if "dbg_k" in io and h < 3:
                    S.dma("sp", lambda g, h=h: g.dma_start(out=io["dbg_k"][h], in_=kT[:]), reads=[bkT], writes=[Buf("x")], owner=bkT)
                    S.dma("sp", lambda g, h=h: g.dma_start(out=io["dbg_v"][h], in_=vT[:]), reads=[bvT], writes=[Buf("x")], owner=bvT)
                    S.dma("sp", lambda g, h=h: g.dma_start(out=io["dbg_V"][h], in_=V[:]), reads=[bV], writes=[Buf("x")], owner=bV)
                w, bw_ = next_slab()
                evac_qz(w, bw_)
                mi = mxc[0] % 2
                mxc[0] += 1
                for qb in range(8):
                    jl = [j for j in range(7) if 0 <= qb - 1 + j <= 11]
                    bmi = qb % 2
                    S.op("pool", lambda g, bi=bi, bmi=bmi, qb=qb: g.tensor_tensor(out=bm[bmi][:], in0=biasB[bi][:],
                                                                                  in1=maskB[:, qb * 896:(qb + 1) * 896], op=ALU.add),
                         reads=[bbias[bi], bmB], writes=[bbm[bmi]])

                    def add_fn(b, cs, bmi=bmi):
                        S.op("dve", lambda g: g.scalar_tensor_tensor(out=Sm[b][:, cs], in0=Sps[:, cs], scalar=SCALE,
                                                                     in1=bm[bmi][:, cs], op0=ALU.mult, op1=ALU.add),
                             reads=[bSps, bbm[bmi]], writes=[bSm[b]])

                    def denom_fn(b, sc_):
                        S.op("dve", lambda g: g.tensor_copy(out=den[b][:], in_=OS[:, sc_]), reads=[bOS[b]], writes=[bden[b]])
                    attend(qb, jl, lambda j, qb=qb: qb - 1 + j, add_fn, denom_fn, mx[mi], bmx[mi])
                S.dma("sp", lambda g, h=h, mi=mi: g.dma_start(out=io["mix_s"][16 + h], in_=mx[mi][:]),
                      reads=[bmx[mi]], writes=[bmix], owner=bmx[mi])
                if "mix_dbg" in io:
                    S.dma("sp", lambda g, h=h, mi=mi: g.dma_start(out=io["mix_dbg"][16 + h], in_=mx[mi][:]),
                          reads=[bmx[mi]], writes=[bmix], owner=bmx[mi])
        S.emit()


def phase_A(nc, io):
    SC = 256
    with contextlib.ExitStack() as st_outer:
        T0, _ = _alloc(nc, st_outer)
        hnT = T0("hnT", [128, NDC, EXT], BF16)
        identb = T0("identb", [128, 128], BF16)
        bhnT = Buf("hnT")
        bident = Buf("identb")
        with contextlib.ExitStack() as st:
            T, P = _alloc(nc, st)
            S = Sched(nc)
            xt = [T("xt%d" % i, [128, D], F32) for i in range(2)]
            hnb = [T("hnb%d" % i, [128, D], BF16) for i in range(2)]
            sqj = T("sqj", [128, D], BF16)
            gb = T("gb", [128, D], F32)
            identf = T("identf", [128, 128], F32)
            ssq = T("ssq", [128, 16], F32)
            sd = T("sd", [128, 16], F32)
            rstd = T("rstd", [128, 16], F32)
            ptr = [P("ptr%d" % i, [128, 1024], BF16) for i in range(2)]
            bxt = [Buf("xt0"), Buf("xt1")]
            bhnb = [Buf("hnb0"), Buf("hnb1")]
            bsqj, bg, bidf, bstat = Buf("sqj"), Buf("gb"), Buf("identf"), Buf("stat")
            bptr = [Buf("ptr0"), Buf("ptr1")]
            S.dma("sp", lambda g: g.dma_start(out=gb[:], in_=io["g0b"][:, :]), writes=[bg], owner=bg)
            S.dma("sp", lambda g: g.dma_start(out=identf[:], in_=io["ident"][:, :]), writes=[bidf], owner=bidf)
            S.op("dve", lambda g: g.tensor_copy(out=identb[:], in_=identf[:]), reads=[bidf], writes=[bident])
            cnt = [0]
            for t in range(EXT // 128):
                b = t % 2
                S.dma("sp", lambda g, t=t, b=b: g.dma_start(out=xt[b][:], in_=io["xe"][t * 128:(t + 1) * 128, :]),
                      writes=[bxt[b]], owner=bxt[b])
                rmsnorm_transpose_tile(S, None, xt[b], bxt[b], hnb[b], bhnb[b], gb, bg, ssq, sd, rstd, Buf("stat%d" % t), t,
                                       identb, bident, ptr, bptr,
                                       lambda grp, t=t: hnT[:, grp * 8:(grp + 1) * 8, t * 128:(t + 1) * 128],
                                       bhnT, sqj, bsqj, RMS_EPS, cnt)
            S.emit()
        for is_A in (True, False):
            if _STOP >= (2 if is_A else 3):
                _mixer_scope(nc, io, is_A, hnT, bhnT, identb, bident)
    if _STOP >= 4:
        outproj_phase(nc, io["mix_s"], io["w1r"], io["xe"], HALO, io["h1s"], io["g1b"], io["ident"], hn_out=io["hn1T"], final_out=None,
                      h_ext=io.get("h1"))


def outproj_phase(nc, mix_src, wr, res_src, res_row0, h_dst, gain_b, ident_in, hn_out=None, final_out=None, h_ext=None):
    SCO = 512
    with contextlib.ExitStack() as st0:
        T0, _ = _alloc(nc, st0)
        rstd = T0("rstd3", [128, 8], F32)
        with contextlib.ExitStack() as st:
            T, P = _alloc(nc, st)
            S = Sched(nc)
            mixT = T("mixT", [128, NDC, TOK], BF16)
            wsl = [T("wo%d" % i, [128, NDC * SCO], BF16) for i in range(2)]
            xres = [T("xres%d" % i, [128, SCO], F32) for i in range(2)]
            h1t = [T("h1t%d" % i, [128, SCO], F32) for i in range(2)]
            sqj = T("sqj2", [128, SCO], BF16)
            ssq2 = T("ssq2", [128, 64], F32)
            ssq = T("ssq3", [128, 8], F32)
            sd = T("sd3", [128, 8], F32)
            PO = [P("PO%d" % i, [128, 512], F32) for i in range(2)]
            bmixT = Buf("mixT")
            bw = [Buf("wo0"), Buf("wo1")]
            bx = [Buf("xres0"), Buf("xres1")]
            bh = [Buf("h1t0"), Buf("h1t1")]
            bsqj, bstat, bhd = Buf("sqj2"), Buf("stat3"), Buf("h_dst")
            bPO = [Buf("PO0"), Buf("PO1")]
            for q in range(4):
                S.dma("sp", lambda g, q=q: g.dma_start(out=mixT[:, q * 8:(q + 1) * 8, :],
                                                       in_=mix_src[q * 8:(q + 1) * 8].rearrange("a p t -> p a t")),
                      writes=[bmixT], owner=bmixT)
            S.op("pool", lambda g: g.memset(ssq2[:], 0.0), writes=[bstat])
            it = 0
            for cb in range(D // SCO):
                wb_ = cb % 2
                S.dma("pool", lambda g, cb=cb, wb_=wb_: g.dma_start(out=wsl[wb_][:], in_=wr[cb]), writes=[bw[wb_]], owner=bw[wb_])
                for tt in range(TOK // 128):
                    b = it % 2
                    it += 1
                    S.dma("sp", lambda g, tt=tt, cb=cb, b=b: g.dma_start(
                        out=xres[b][:], in_=res_src[res_row0 + tt * 128: res_row0 + (tt + 1) * 128, cb * SCO:(cb + 1) * SCO]),
                        writes=[bx[b]], owner=bx[b])
                    for ec in range(NDC):
                        S.op("pe", lambda g, ec=ec, tt=tt, wb_=wb_, b=b: g.matmul(PO[b][:], lhsT=mixT[:, ec, tt * 128:(tt + 1) * 128],
                                                                                rhs=wsl[wb_][:, ec * SCO:(ec + 1) * SCO],
                                                                                start=(ec == 0), stop=(ec == NDC - 1)),
                             reads=[bmixT, bw[wb_]], writes=[bPO[b]])
                    S.op("dve", lambda g, b=b: g.tensor_tensor(out=h1t[b][:], in0=PO[b][:], in1=xres[b][:], op=ALU.add),
                         reads=[bPO[b], bx[b]], writes=[bh[b]])
                    col = tt * 8 + cb
                    S.op("act", lambda g, b=b, col=col: g.activation(out=sqj[:], in_=h1t[b][:], func=AF.Square,
                                                                     accum_out=ssq2[:, col:col + 1]),
                         reads=[bh[b]], writes=[bsqj, bstat])
                    S.dma("sp", lambda g, tt=tt, cb=cb, b=b: g.dma_start(
                        out=h_dst[tt * 128:(tt + 1) * 128, cb * SCO:(cb + 1) * SCO], in_=h1t[b][:]),
                        reads=[bh[b]], writes=[bhd], owner=bh[b])
                    if h_ext is not None:
                        S.dma("sp", lambda g, tt=tt, cb=cb, b=b: g.dma_start(
                            out=h_ext[tt * 128:(tt + 1) * 128, cb * SCO:(cb + 1) * SCO], in_=h1t[b][:]),
                            reads=[bh[b]], writes=[bhd], owner=bh[b])
            S.op("dve", lambda g: g.tensor_reduce(out=ssq[:], in_=ssq2[:].rearrange("p (a b) -> p a b", a=8),
                                                  axis=mybir.AxisListType.X, op=ALU.add), reads=[bstat], writes=[bstat])
            S.op("act", lambda g: g.activation(out=sd[:], in_=ssq[:], func=AF.Sqrt, scale=1.0 / D, bias=RMS_EPS),
                 reads=[bstat], writes=[bstat])
            S.op("dve", lambda g: g.reciprocal(out=rstd[:], in_=sd[:]), reads=[bstat], writes=[bstat])
            S.emit()
        if _STOP < 5:
            return
        with contextlib.ExitStack() as st2:
            T2, P2 = _alloc(nc, st2)
            S = Sched(nc)
            hrow = [T2("hrow%d" % i, [128, D], F32) for i in range(2)]
            gb = T2("gb2", [128, D], F32)
            bhrow = [Buf("hrow0"), Buf("hrow1")]
            bg = Buf("gb2")
            brst = Buf("rstd3")
            bsrc = Buf("hsrc")
            S.dma("sp", lambda g: g.dma_start(out=gb[:], in_=gain_b[:, :]), writes=[bg], owner=bg)
            if final_out is not None:
                orow = [T2("orow%d" % i, [128, D], F32) for i in range(2)]
                borow = [Buf("orow0"), Buf("orow1")]
                bout = Buf("final_out")
                for tt in range(TOK // 128):
                    b = tt % 2
                    S.dma("sp", lambda g, tt=tt, b=b: g.dma_start(out=hrow[b][:], in_=h_dst[tt * 128:(tt + 1) * 128, :]),
                          reads=[bsrc], writes=[bhrow[b]], owner=bhrow[b])
                    S.op("dve", lambda g, tt=tt, b=b: g.scalar_tensor_tensor(out=orow[b][:], in0=hrow[b][:], scalar=rstd[:, tt:tt + 1],
                                                                            in1=gb[:], op0=ALU.mult, op1=ALU.mult),
                         reads=[bhrow[b], bg, brst], writes=[borow[b]])
                    S.dma("sp", lambda g, tt=tt, b=b: g.dma_start(out=final_out[tt * 128:(tt + 1) * 128, :], in_=orow[b][:]),
                          reads=[borow[b]], writes=[bout], owner=borow[b])
            else:
                hnb = [T2("hnb2_%d" % i, [128, D], BF16) for i in range(2)]
                hnTt = [T2("hnTt%d" % i, [128, NDC, 128], BF16) for i in range(2)]
                identf = T2("identf2", [128, 128], F32)
                identb = T2("identb2", [128, 128], BF16)
                ptr = [P2("ptr2_%d" % i, [128, 1024], BF16) for i in range(2)]
                bhnb = [Buf("hnb0"), Buf("hnb1")]
                bhnTt = [Buf("hnTt0"), Buf("hnTt1")]
                bidf, bident = Buf("idf"), Buf("idb")
                bptr = [Buf("ptr0"), Buf("ptr1")]
                bout = Buf("hn_out")
                S.dma("sp", lambda g: g.dma_start(out=identf[:], in_=ident_in[:, :]), writes=[bidf], owner=bidf)
                S.op("dve", lambda g: g.tensor_copy(out=identb[:], in_=identf[:]), reads=[bidf], writes=[bident])
                cnt = 0
                for tt in range(TOK // 128):
                    b = tt % 2
                    S.dma("sp", lambda g, tt=tt, b=b: g.dma_start(out=hrow[b][:], in_=h_dst[tt * 128:(tt + 1) * 128, :]),
                          reads=[bsrc], writes=[bhrow[b]], owner=bhrow[b])
                    S.op("dve", lambda g, tt=tt, b=b: g.scalar_tensor_tensor(out=hnb[b][:], in0=hrow[b][:], scalar=rstd[:, tt:tt + 1],
                                                                            in1=gb[:], op0=ALU.mult, op1=ALU.mult),
                         reads=[bhrow[b], bg, brst], writes=[bhnb[b]])
                    for grp in range(4):
                        pb = cnt % 2
                        cnt += 1
                        for i in range(8):
                            dc = grp * 8 + i
                            S.op("pe", lambda g, dc=dc, i=i, pb=pb, b=b: g.transpose(out=ptr[pb][:, i * 128:(i + 1) * 128],
                                                                                   in_=hnb[b][:, dc * 128:(dc + 1) * 128],
                                                                                   identity=identb[:]),
                                 reads=[bhnb[b], bident], writes=[bptr[pb]])
                        src = ptr[pb][:].rearrange("p (a b) -> p a b", a=8)
                        if grp % 2 == 0:
                            S.op("act", lambda g, grp=grp, src=src, b=b: g.copy(out=hnTt[b][:, grp * 8:(grp + 1) * 8, :], in_=src),
                                 reads=[bptr[pb]], writes=[bhnTt[b]])
                        else:
                            S.op("dve", lambda g, grp=grp, src=src, b=b: g.tensor_copy(out=hnTt[b][:, grp * 8:(grp + 1) * 8, :], in_=src),
                                 reads=[bptr[pb]], writes=[bhnTt[b]])
                    S.dma("sp", lambda g, tt=tt, b=b: g.dma_start(
                        out=hn_out[:, :, tt * 128:(tt + 1) * 128].rearrange("a p t -> p a t"), in_=hnTt[b][:]),
                        reads=[bhnTt[b]], writes=[bout], owner=bhnTt[b])
            S.emit()


def phase_B(nc, io):
    SC = 256
    NTB = SEQ // 512
    NKB = SEQ // 128
    with contextlib.ExitStack() as st:
        T, P = _alloc(nc, st)
        S = Sched(nc)
        wk = T("wk", [128, NDC * SC], BF16)
        wv = T("wv", [128, NDC * SC], BF16)
        wq, wz = wk, wv
        hnb = [T("hnblk%d" % i, [128, NDC, 512], BF16) for i in range(2)]
        cosb = [T("cosb%d" % i, [128, 512], F32) for i in range(2)]
        sinb = [T("sinb%d" % i, [128, 512], F32) for i in range(2)]
        KT = [T("KT%d" % i, [128, SEQ], BF16) for i in range(2)]
        V = T("Vres", [128, NKB, 256], BF16)
        permf = T("permfB", [128, 128], F32)
        identf = T("identfB", [128, 128], F32)
        identb = T("identbB", [128, 128], BF16)
        ones = T("onesB", [128, 128], BF16)
        onesf = T("onesfB", [128, 128], F32)
        lst = T("lst", [128, 8], F32)
        gsub = T("gsub", [128, 2], F32)
        gsubs = T("gsubs", [128, 2], F32)
        kf = T("kfB", [128, 512], F32)
        t1 = T("t1B", [128, 512], F32)
        lamt = t1
        lprod = kf
        t2 = T("t2B", [128, 512], F32)
        vTb = T("vTb", [128, 512], BF16)
        QT = [[T("QT%d_%d" % (i, p), [128, 512], BF16) for p in range(2)] for i in range(2)]
        zs = [[T("zs%d_%d" % (i, c), [128, 512], F32) for c in range(2)] for i in range(2)]
        NPT = 3
        Pt = [T("PtB%d" % i, [128, 512], BF16) for i in range(NPT)]
        On0 = [T("On0_%d" % c, [128, 512], F32) for c in range(2)]
        tmpB = [T("tmpB%d" % c, [128, 512], F32) for c in range(2)]
        sdB = T("sdB", [128, 512], F32)
        recs = sdB
        rsB = sdB
        ob = [T("ob_%d" % c, [128, 512], BF16) for c in range(2)]
        PA = [P("PAB%d" % i, [128, 512], F32) for i in range(2)]
        ROT = P("ROTB", [128, 512], F32)
        Sp = [P("SpB%d" % i, [128, 512], F32) for i in range(2)]
        Op_ = [P("OpB%d" % i, [128, 512], F32) for i in range(2)]
        SM = P("SMB", [128, 512], F32)

        bwk, bwv = Buf("wk"), Buf("wv")
        bwq, bwz = bwk, bwv
        bhn = [Buf("hnblk0"), Buf("hnblk1")]
        btab = [Buf("tab0"), Buf("tab1")]
        bKT = [Buf("KT0"), Buf("KT1")]
        bV = Buf("V")
        bperm, bidf, bident, bones, bonesf = Buf("perm"), Buf("idf"), Buf("idb"), Buf("ones"), Buf("onesf")
        bgs = Buf("gsub")
        bkf, bt1, bt2, bvTb = Buf("kf"), Buf("t1"), Buf("t2"), Buf("vTb")
        blam, blst = bt1, Buf("lst")
        bQT = [[Buf("QT%d_%d" % (i, p)) for p in range(2)] for i in range(2)]
        bzs = [[Buf("zs%d_%d" % (i, c)) for c in range(2)] for i in range(2)]
        bPt = [Buf("Pt%d" % i) for i in range(NPT)]
        bOn0 = [Buf("On0_0"), Buf("On0_1")]
        btmp = [Buf("tmp0"), Buf("tmp1")]
        bsd = Buf("sdB")
        brecs = bsd
        brs = bsd
        bob = [Buf("ob0"), Buf("ob1")]
        bPA = [Buf("PA0"), Buf("PA1")]
        bROT = Buf("ROT")
        bSp = [Buf("Sp0"), Buf("Sp1")]
        bOp = [Buf("Op0"), Buf("Op1")]
        bSM = Buf("SM")
        bout = Buf("oT_out")

        S.dma("sp", lambda g: g.dma_start(out=permf[:], in_=io["perm"][:, :]), writes=[bperm], owner=bperm)
        S.dma("sp", lambda g: g.dma_start(out=identf[:], in_=io["ident"][:, :]), writes=[bidf], owner=bidf)
        S.op("dve", lambda g: g.tensor_copy(out=identb[:], in_=identf[:]), reads=[bidf], writes=[bident])
        S.op("pool", lambda g: g.memset(ones[:], 1.0), writes=[bones])
        S.op("pool", lambda g: g.memset(onesf[:], 1.0), writes=[bonesf])
        S.dma("sp", lambda g: g.dma_start(out=lamt[:], in_=io["lamb"][:, :]), writes=[blam], owner=blam)
        S.dma("sp", lambda g: g.dma_start(out=gsub[:], in_=io["gsub"][:, :]), writes=[bgs], owner=bgs)
        for i in range(2):
            S.op("dve", lambda g, i=i: g.tensor_tensor(out=lprod[:, 0:128], in0=lamt[:, (2 * i) * 128:(2 * i + 1) * 128],
                                                       in1=lamt[:, (2 * i + 1) * 128:(2 * i + 2) * 128], op=ALU.mult),
                 reads=[blam], writes=[bkf])
            S.op("dve", lambda g, i=i: g.tensor_reduce(out=lst[:, i:i + 1], in_=lprod[:, 0:128], axis=mybir.AxisListType.X, op=ALU.add),
                 reads=[bkf], writes=[blst])
        S.op("act", lambda g: g.activation(out=lst[:, 2:4], in_=lst[:, 0:2], func=AF.Exp), reads=[blst], writes=[blst])
        S.op("dve", lambda g: g.tensor_tensor(out=lst[:, 4:5], in0=lst[:, 3:4], in1=lst[:, 2:3], op=ALU.subtract),
             reads=[blst], writes=[blst])
        S.op("dve", lambda g: g.tensor_scalar(out=lst[:, 4:5], in0=lst[:, 4:5], scalar1=-LAMBDA_INIT, scalar2=None, op0=ALU.add),
             reads=[blst], writes=[blst])
        S.op("dve", lambda g: g.tensor_scalar(out=gsubs[:], in0=gsub[:], scalar1=1.0 - LAMBDA_INIT, scalar2=None, op0=ALU.mult),
             reads=[bgs], writes=[bgs])

        ldc = [0]

        def load_block(tb):
            b = ldc[0] % 2
            ldc[0] += 1
            r, off = tb // 2, (tb % 2) * 512
            for q in range(4):
                S.dma("sp", lambda g, q=q, b=b, r=r, off=off: g.dma_start(
                    out=hnb[b][:, q * 8:(q + 1) * 8, :],
                    in_=io["hg"][r, q * 8:(q + 1) * 8, :, off:off + 512].rearrange("a p t -> p a t")),
                    writes=[bhn[b]], owner=bhn[b])
            S.dma("sp", lambda g, b=b, tb=tb: g.dma_start(out=cosb[b][:], in_=io["cosB"][:, tb * 512:(tb + 1) * 512]),
                  writes=[btab[b]], owner=btab[b])
            S.dma("sp", lambda g, b=b, tb=tb: g.dma_start(out=sinb[b][:], in_=io["sinB"][:, tb * 512:(tb + 1) * 512]),
                  writes=[btab[b]], owner=btab[b])
            return b

        pac = [0]

        def proj1(w, bw_, ft, b):
            pi = pac[0] % 2
            pac[0] += 1
            for dc in range(NDC):
                S.op("pe", lambda g, dc=dc, pi=pi: g.matmul(PA[pi][:], lhsT=w[:, dc * SC + ft * 128: dc * SC + ft * 128 + 128],
                                                           rhs=hnb[b][:, dc, :], start=(dc == 0), stop=(dc == NDC - 1)),
                     reads=[bw_, bhn[b]], writes=[bPA[pi]])
            return pi

        def proj_qz(hh, tb, slot):
            b = load_block(tb)
            for part in range(2):
                pi = proj1(wq, bwq, part, b)
                rope_block(S, PA[pi][:], bPA[pi], kf[:], bkf, permf, bperm, ROT[:], bROT, cosb[b][:], sinb[b][:], btab[b],
                           t1[:], bt1, t2[:], bt2, QT[slot][part][:], bQT[slot][part])
            for c in range(2):
                pi = proj1(wz, bwz, c, b)
                S.op("act", lambda g, pi=pi, c=c: g.activation(out=zs[slot][c][:], in_=PA[pi][:], func=AF.Silu),
                     reads=[bPA[pi]], writes=[bzs[slot][c]])

        ptc = [0]

        def attn_part(slot, part):
            def qk(kb):
                sb = kb % 2
                S.op("pe", lambda g: g.matmul(Sp[sb][:], lhsT=KT[part][:, kb * 128:(kb + 1) * 128], rhs=QT[slot][part][:],
                                              start=True, stop=True), reads=[bKT[part], bQT[slot][part]], writes=[bSp[sb]])
            qk(0)
            for kb in range(NKB):
                sb = kb % 2
                pt = ptc[0] % NPT
                ptc[0] += 1
                S.op("act", lambda g, sb=sb, pt=pt: g.activation(out=Pt[pt][:], in_=Sp[sb][:], func=AF.Exp, scale=SCALE),
                     reads=[bSp[sb]], writes=[bPt[pt]])
                if kb + 1 < NKB:
                    qk(kb + 1)
                for c in range(2):
                    S.op("pe", lambda g, kb=kb, c=c, pt=pt: g.matmul(Op_[c][:], lhsT=V[:, kb, c * 128:(c + 1) * 128], rhs=Pt[pt][:],
                                                                    start=(kb == 0), stop=(kb == NKB - 1)),
                         reads=[bV, bPt[pt]], writes=[bOp[c]])
                S.op("pe", lambda g, kb=kb, pt=pt: g.matmul(SM[:], lhsT=ones[:], rhs=Pt[pt][:], start=(kb == 0), stop=(kb == NKB - 1)),
                     reads=[bones, bPt[pt]], writes=[bSM])
            S.op("dve", lambda g: g.reciprocal(out=recs[:], in_=SM[:]), reads=[bSM], writes=[brecs])
            for c in range(2):
                dst, bdst = (On0[c], bOn0[c]) if part == 0 else (tmpB[c], btmp[c])
                S.op("dve", lambda g, c=c, dst=dst: g.tensor_tensor(out=dst[:], in0=Op_[c][:], in1=recs[:], op=ALU.mult),
                     reads=[bOp[c], brecs], writes=[bdst])

        def finalize(hh, qb, slot):
            for c in range(2):
                S.op("dve", lambda g, c=c: g.scalar_tensor_tensor(out=On0[c][:], in0=tmpB[c][:], scalar=lst[:, 4:5], in1=On0[c][:],
                                                                  op0=ALU.mult, op1=ALU.add),
                     reads=[btmp[c], bOn0[c], blst], writes=[bOn0[c]])
                S.op("pool", lambda g, c=c: g.tensor_tensor(out=tmpB[c][:], in0=On0[c][:], in1=On0[c][:], op=ALU.mult),
                     reads=[bOn0[c]], writes=[btmp[c]])
            for c in range(2):
                S.op("pe", lambda g, c=c: g.matmul(ROT[:], lhsT=onesf[:], rhs=tmpB[c][:], start=(c == 0), stop=(c == 1)),
                     reads=[bonesf, btmp[c]], writes=[bROT])
            S.op("act", lambda g: g.activation(out=sdB[:], in_=ROT[:], func=AF.Sqrt, scale=1.0 / 256, bias=SUBLN_EPS),
                 reads=[bROT], writes=[bsd])
            S.op("dve", lambda g: g.reciprocal(out=rsB[:], in_=sdB[:]), reads=[bsd], writes=[brs])
            for c in range(2):
                S.op("dve", lambda g, c=c: g.scalar_tensor_tensor(out=tmpB[c][:], in0=On0[c][:], scalar=gsubs[:, c:c + 1], in1=rsB[:],
                                                                  op0=ALU.mult, op1=ALU.mult),
                     reads=[bOn0[c], bgs, brs], writes=[btmp[c]])
                S.op("pool", lambda g, c=c: g.tensor_tensor(out=ob[c][:], in0=tmpB[c][:], in1=zs[slot][c][:], op=ALU.mult),
                     reads=[btmp[c], bzs[slot][c]], writes=[bob[c]])
                S.dma("sp", lambda g, c=c: g.dma_start(out=io["oT"][hh * 2 + c, :, qb * 512:(qb + 1) * 512], in_=ob[c][:]),
                      reads=[bob[c]], writes=[bout], owner=bob[c])

        for hh in range(2):
            S.dma("pool", lambda g, hh=hh: g.dma_start(out=wk[:], in_=io["wBr"][hh, 0]), writes=[bwk], owner=bwk)
            S.dma("pool", lambda g, hh=hh: g.dma_start(out=wv[:], in_=io["wBr"][hh, 1]), writes=[bwv], owner=bwv)
            for tb in range(NTB):
                b = load_block(tb)
                sl = slice(tb * 512, (tb + 1) * 512)
                for part in range(2):
                    pi = proj1(wk, bwk, part, b)
                    rope_block(S, PA[pi][:], bPA[pi], kf[:], bkf, permf, bperm, ROT[:], bROT, cosb[b][:], sinb[b][:], btab[b],
                               t1[:], bt1, t2[:], bt2, KT[part][:, sl], bKT[part])
                for c in range(2):
                    pi = proj1(wv, bwv, c, b)
                    S.op("act", lambda g, pi=pi: g.copy(out=vTb[:], in_=PA[pi][:]), reads=[bPA[pi]], writes=[bvTb])
                    trb = ROT[:].bitcast(BF16)
                    for i in range(4):
                        S.op("pe", lambda g, i=i, trb=trb: g.transpose(out=trb[:, i * 128:(i + 1) * 128], in_=vTb[:, i * 128:(i + 1) * 128],
                                                                       identity=identb[:]), reads=[bvTb, bident], writes=[bROT])
                    S.op("dve", lambda g, tb=tb, c=c, trb=trb: g.tensor_copy(
                        out=V[:, tb * 4:(tb + 1) * 4, c * 128:(c + 1) * 128],
                        in_=trb[:, 0:512].rearrange("p (a b) -> p a b", a=4)), reads=[bROT], writes=[bV])
            S.dma("pool", lambda g, hh=hh: g.dma_start(out=wq[:], in_=io["wBr"][hh, 2]), writes=[bwq], owner=bwq)
            S.dma("pool", lambda g, hh=hh: g.dma_start(out=wz[:], in_=io["wBr"][hh, 3]), writes=[bwz], owner=bwz)
            proj_qz(hh, 0, 0)
            for qb in range(NTB):
                slot = qb % 2
                attn_part(slot, 0)
                if qb + 1 < NTB:
                    proj_qz(hh, qb + 1, 1 - slot)
                attn_part(slot, 1)
                finalize(hh, qb, slot)
        S.emit()


def _dt(nc, name, shape, dt, kind):
    return nc.dram_tensor(name, list(shape), dt, kind=kind).ap()


def build_A():
    nc = bass.Bass("TRN2", target_bir_lowering=False)
    with contextlib.ExitStack() as gst:
        _BANK[id(nc)] = SemBank(nc, gst)
        io = {}
        io["xe"] = _dt(nc, "xe", [EXT, D], F32, "ExternalInput")
        io["g0b"] = _dt(nc, "g0b", [128, D], F32, "ExternalInput")
        io["g1b"] = _dt(nc, "g1b", [128, D], F32, "ExternalInput")
        io["ident"] = _dt(nc, "ident", [128, 128], F32, "ExternalInput")
        io["perm"] = _dt(nc, "perm", [128, 128], F32, "ExternalInput")
        io["w0r"] = _dt(nc, "w0r", [52, 128, NDC * 256], F32, "ExternalInput")
        io["w1r"] = _dt(nc, "w1r", [8, 128, NDC * 512], F32, "ExternalInput")
        io["cosA"] = _dt(nc, "cosA", [128, EXT], F32, "ExternalInput")
        io["sinA"] = _dt(nc, "sinA", [128, EXT], F32, "ExternalInput")
        io["maskW"] = _dt(nc, "maskW", [128, 8 * 384], BF16, "ExternalInput")
        io["maskB"] = _dt(nc, "maskB", [128, 8 * 896], BF16, "ExternalInput")
        io["biasB"] = _dt(nc, "biasB", [16, 128, 896], F32, "ExternalInput")
        io["sinkb"] = _dt(nc, "sinkb", [128, 16], F32, "ExternalInput")
        io["mix_s"] = _dt(nc, "mix_s", [NDC, 128, TOK], BF16, "Internal")
        io["h1s"] = _dt(nc, "h1s", [TOK, D], F32, "Internal")
        if _DBG:
            io["mix_dbg"] = _dt(nc, "mix_dbg", [NDC, 128, TOK], BF16, "ExternalOutput")
            io["dbg_k"] = _dt(nc, "dbg_k", [3, 128, EXT], BF16, "ExternalOutput")
            io["dbg_v"] = _dt(nc, "dbg_v", [3, 128, EXT], BF16, "ExternalOutput")
            io["dbg_V"] = _dt(nc, "dbg_V", [3, 128, EXT // 128, 128], BF16, "ExternalOutput")
        io["h1"] = _dt(nc, "h1", [TOK, D], F32, "ExternalOutput")
        io["hn1T"] = _dt(nc, "hn1T", [NDC, 128, TOK], BF16, "ExternalOutput")
        phase_A(nc, io)
    return nc


def build_B():
    nc = bass.Bass("TRN2", target_bir_lowering=False)
    with contextlib.ExitStack() as gst:
        _BANK[id(nc)] = SemBank(nc, gst)
        io = {}
        io["hg"] = _dt(nc, "hg", [NCORES, NDC, 128, TOK], BF16, "ExternalInput")
        io["wBr"] = _dt(nc, "wBr", [2, 4, 128, NDC * 256], F32, "ExternalInput")
        io["cosB"] = _dt(nc, "cosB", [128, SEQ], F32, "ExternalInput")
        io["sinB"] = _dt(nc, "sinB", [128, SEQ], F32, "ExternalInput")
        io["ident"] = _dt(nc, "ident", [128, 128], F32, "ExternalInput")
        io["perm"] = _dt(nc, "perm", [128, 128], F32, "ExternalInput")
        io["lamb"] = _dt(nc, "lamb", [128, 512], F32, "ExternalInput")
        io["gsub"] = _dt(nc, "gsub", [128, 2], F32, "ExternalInput")
        io["oT"] = _dt(nc, "oT", [4, 128, SEQ], BF16, "ExternalOutput")
        phase_B(nc, io)
    return nc


def build_C():
    nc = bass.Bass("TRN2", target_bir_lowering=False)
    with contextlib.ExitStack() as gst:
        _BANK[id(nc)] = SemBank(nc, gst)
        io = {}
        io["og"] = _dt(nc, "og", [NDC, 128, TOK], BF16, "ExternalInput")
        io["h1in"] = _dt(nc, "h1in", [TOK, D], F32, "ExternalInput")
        io["w2r"] = _dt(nc, "w2r", [8, 128, NDC * 512], F32, "ExternalInput")
        io["gfb"] = _dt(nc, "gfb", [128, D], F32, "ExternalInput")
        io["ident"] = _dt(nc, "ident", [128, 128], F32, "ExternalInput")
        io["h2s"] = _dt(nc, "h2s", [TOK, D], F32, "Internal")
        io["out"] = _dt(nc, "out", [TOK, D], F32, "ExternalOutput")
        outproj_phase(nc, io["og"], io["w2r"], io["h1in"], 0, io["h2s"], io["gfb"], io["ident"], hn_out=None, final_out=io["out"])
    return nc


def _host_consts():
    ident = np.eye(128, dtype=np.float32)
    perm = np.zeros((128, 128), np.float32)
    for j in range(128):
        perm[(j + 64) % 128, j] = 1.0
    return ident, perm


def prep_A(x, norm_g, w_in_even, sink_a, rpb_b, w_out_even):
    ident, perm = _host_consts()
    cos, sin = _rope_tables()
    x2 = np.asarray(x, np.float32).reshape(SEQ, D)
    xpad = np.zeros((SEQ + 2 * HALO, D), np.float32)
    xpad[HALO:HALO + SEQ] = x2
    cpad = np.zeros((SEQ + 2 * HALO, 128), np.float32)
    spad = np.zeros((SEQ + 2 * HALO, 128), np.float32)
    cpad[HALO:HALO + SEQ] = cos
    spad[HALO:HALO + SEQ] = sin
    w0 = np.asarray(w_in_even, np.float32)[0][:, _even_col_order()]
    w0r = _slabs(w0, 256)
    w1r = _slabs(np.asarray(w_out_even, np.float32)[0], 512)
    g0b = _bcast(np.asarray(norm_g)[0])
    g1b = _bcast(np.asarray(norm_g)[1])
    sinkb = _bcast(np.asarray(sink_a)[0])
    rpb = np.asarray(rpb_b, np.float32)[0]
    maps = []
    for c in range(NCORES):
        mB, bB = _natten_tables(c, rpb)
        maps.append({
            "xe": np.ascontiguousarray(xpad[c * TOK: c * TOK + EXT]),
            "g0b": g0b, "g1b": g1b, "ident": ident, "perm": perm,
            "w0r": w0r, "w1r": w1r,
            "cosA": np.ascontiguousarray(cpad[c * TOK: c * TOK + EXT].T),
            "sinA": np.ascontiguousarray(spad[c * TOK: c * TOK + EXT].T),
            "maskW": _window_mask(c).astype(ml_dtypes.bfloat16),
            "maskB": mB.astype(ml_dtypes.bfloat16),
            "biasB": bB,
            "sinkb": sinkb,
        })
    return maps


def prep_B(hn1T_all, w_in_odd, lq1, lk1, lq2, lk2, subln_g):
    ident, perm = _host_consts()
    cos, sin = _rope_tables()
    cosB = np.ascontiguousarray(cos.T)
    sinB = np.ascontiguousarray(sin.T)
    hg = np.ascontiguousarray(np.stack(hn1T_all, axis=0))
    w = np.asarray(w_in_odd, np.float32)[0]
    lamb = np.concatenate([_bcast(np.asarray(v)[0]) for v in (lq1, lk1, lq2, lk2)], axis=1)
    gsub = np.ascontiguousarray(np.asarray(subln_g, np.float32)[0].reshape(2, 128).T)
    maps = []
    for c in range(NCORES):
        slabs = []
        for hh in range(2):
            h = 2 * c + hh
            per = []
            for base in (4096, 8192, 0, 12288):
                per.append(_slabs(w[:, base + h * 256: base + (h + 1) * 256], 256)[0])
            slabs.append(np.stack(per, axis=0))
        maps.append({"hg": hg, "wBr": np.ascontiguousarray(np.stack(slabs, axis=0)), "cosB": cosB, "sinB": sinB,
                     "ident": ident, "perm": perm, "lamb": lamb, "gsub": gsub})
    return maps


def prep_C(oT_all, h1_all, w_out_odd, final_g):
    ident, _ = _host_consts()
    w2r = _slabs(np.asarray(w_out_odd, np.float32)[0], 512)
    gfb = _bcast(np.asarray(final_g))
    maps = []
    for c in range(NCORES):
        og = np.concatenate([oT_all[r][:, :, c * TOK:(c + 1) * TOK] for r in range(NCORES)], axis=0)
        maps.append({"og": np.ascontiguousarray(og), "h1in": h1_all[c], "w2r": w2r, "gfb": gfb, "ident": ident})
    return maps


def kernel(x, norm_g, w_in_even, sink_a, rpb_b, w_out_even, w_in_odd,
           lambda_q1, lambda_k1, lambda_q2, lambda_k2, subln_g, w_out_odd, final_g):
    cores = list(range(NCORES))
    ncA = build_A()
    resA = run_bass_kernel_spmd(ncA, prep_A(x, norm_g, w_in_even, sink_a, rpb_b, w_out_even), core_ids=cores)
    h1_all = [np.asarray(r["h1"]) for r in resA.results]
    hn_all = [np.asarray(r["hn1T"]) for r in resA.results]
    ncB = build_B()
    resB = run_bass_kernel_spmd(ncB, prep_B(hn_all, w_in_odd, lambda_q1, lambda_k1, lambda_q2, lambda_k2, subln_g), core_ids=cores)
    oT_all = [np.asarray(r["oT"]) for r in resB.results]
    ncC = build_C()
    resC = run_bass_kernel_spmd(ncC, prep_C(oT_all, h1_all, w_out_odd, final_g), core_ids=cores)
    out = np.concatenate([np.asarray(r["out"]) for r in resC.results], axis=0)
    return out.reshape(1, SEQ, D).astype(np.float32)
```
